# Optimizing a Trainium2 kernel written in Bass

```python
import math
import jax, jax.numpy as jnp
from jax import lax
import numpy as np

D_MODEL = 2048
BATCH = 2
SEQ = 8192
DEPTH = 2

CHUNK = 64
N_MIXERS = 2
EPS = 1e-6

A_HEADS = 16
A_HEAD_DIM = 128
A_INNER = A_HEADS * A_HEAD_DIM
CONV_K = 4

B_HEADS = 16
B_HEAD_DIM = 128
B_INNER = B_HEADS * B_HEAD_DIM
LEFT_CHUNKS = 8
BAND = (LEFT_CHUNKS + 1) * CHUNK
REL_CLIP = 256

kernel_name = "hybrid_gdn_chunkband_stream"


def rms_norm(x, w):
    xf = x.astype(jnp.float32)
    y = xf * lax.rsqrt(jnp.mean(xf * xf, axis=-1, keepdims=True) + EPS)
    return (y * w.astype(jnp.float32)).astype(x.dtype)


def l2_norm(x):
    xf = x.astype(jnp.float32)
    return xf * lax.rsqrt(jnp.sum(xf * xf, axis=-1, keepdims=True) + EPS)


def causal_depthwise_conv(x, w):
    c = x.shape[-1]
    return lax.conv_general_dilated(
        x, w[:, None, :].astype(x.dtype), window_strides=(1,),
        padding=[(CONV_K - 1, 0)], dimension_numbers=("NWC", "WIO", "NWC"),
        feature_group_count=c)


def gated_deltanet(h, w_in, conv_w, a_log, dt_bias, out_norm_w, w_out):
    bsz, t, _ = h.shape
    nc = t // CHUNK
    proj = h @ w_in.astype(h.dtype)
    qkv = proj[..., :3 * A_INNER]
    z = proj[..., 3 * A_INNER:4 * A_INNER]
    a_in = proj[..., 4 * A_INNER:4 * A_INNER + A_HEADS]
    b_in = proj[..., 4 * A_INNER + A_HEADS:]
    qkv = jax.nn.silu(causal_depthwise_conv(qkv, conv_w))
    q = qkv[..., :A_INNER].reshape(bsz, t, A_HEADS, A_HEAD_DIM)
    k = qkv[..., A_INNER:2 * A_INNER].reshape(bsz, t, A_HEADS, A_HEAD_DIM)
    v = qkv[..., 2 * A_INNER:].reshape(bsz, t, A_HEADS, A_HEAD_DIM).astype(jnp.float32)
    q = l2_norm(q) * (A_HEAD_DIM ** -0.5)
    k = l2_norm(k)
    beta = jax.nn.sigmoid(b_in.astype(jnp.float32))
    g = -jnp.exp(a_log.astype(jnp.float32)) * jax.nn.softplus(
        a_in.astype(jnp.float32) + dt_bias.astype(jnp.float32))

    def to_chunks(u):
        u = u.reshape((bsz, nc, CHUNK) + u.shape[2:])
        return jnp.moveaxis(u, 3, 1)

    q, k, v = to_chunks(q), to_chunks(k), to_chunks(v)
    beta, g = to_chunks(beta), to_chunks(g)
    gc = jnp.cumsum(g, axis=-1)
    tri_incl = jnp.tril(jnp.ones((CHUNK, CHUNK), dtype=bool))
    tri_strict = jnp.tril(jnp.ones((CHUNK, CHUNK), dtype=bool), k=-1)
    diff = gc[..., :, None] - gc[..., None, :]
    decay = jnp.exp(jnp.where(tri_incl, diff, -jnp.inf))

    k_beta = k * beta[..., None]
    v_beta = v * beta[..., None]
    lower = jnp.where(tri_strict, jnp.einsum('bhncd,bhnsd->bhncs', k_beta, k) * decay, 0.0)
    eye = jnp.eye(CHUNK, dtype=jnp.float32)
    rhs = jnp.concatenate([v_beta, k_beta * jnp.exp(gc)[..., None]], axis=-1)
    sol = lax.linalg.triangular_solve(eye + lower, rhs, left_side=True, lower=True,
                                      unit_diagonal=True)
    u = sol[..., :A_HEAD_DIM]
    w = sol[..., A_HEAD_DIM:]
    qk = jnp.einsum('bhncd,bhnsd->bhncs', q, k) * decay
    q_dec = q * jnp.exp(gc)[..., None]
    k_dec = k * jnp.exp(gc[..., -1:] - gc)[..., None]
    g_last = jnp.exp(gc[..., -1])

    def step(state, inp):
        qk_n, q_dec_n, k_dec_n, u_n, w_n, gl_n = inp
        v_new = u_n - jnp.einsum('bhcd,bhdv->bhcv', w_n, state)
        o_n = (jnp.einsum('bhcd,bhdv->bhcv', q_dec_n, state)
               + jnp.einsum('bhcs,bhsv->bhcv', qk_n, v_new))
        state = state * gl_n[..., None, None] + jnp.einsum('bhcd,bhcv->bhdv', k_dec_n, v_new)
        return state, o_n

    xs = tuple(jnp.moveaxis(a, 2, 0) for a in (qk, q_dec, k_dec, u, w, g_last))
    s0 = jnp.zeros((bsz, A_HEADS, A_HEAD_DIM, A_HEAD_DIM), jnp.float32)
    _, o = lax.scan(step, s0, xs)
    o = jnp.transpose(o, (1, 0, 3, 2, 4)).reshape(bsz, t, A_HEADS, A_HEAD_DIM)
    zg = jax.nn.silu(z.astype(jnp.float32)).reshape(bsz, t, A_HEADS, A_HEAD_DIM)
    o = rms_norm(o, out_norm_w) * zg
    return o.reshape(bsz, t, A_INNER).astype(h.dtype) @ w_out.astype(h.dtype)


def chunk_band_attention(h, w_in, q_norm_w, k_norm_w, rel_bias, w_out):
    bsz, t, _ = h.shape
    nc = t // CHUNK
    pad = LEFT_CHUNKS * CHUNK
    proj = h @ w_in.astype(h.dtype)
    q = rms_norm(proj[..., :B_INNER].reshape(bsz, t, B_HEADS, B_HEAD_DIM), q_norm_w)
    k = rms_norm(proj[..., B_INNER:2 * B_INNER].reshape(bsz, t, B_HEADS, B_HEAD_DIM), k_norm_w)
    v = proj[..., 2 * B_INNER:3 * B_INNER].reshape(bsz, t, B_HEADS, B_HEAD_DIM)
    z = proj[..., 3 * B_INNER:]
    k_pad = jnp.pad(k, ((0, 0), (pad, 0), (0, 0), (0, 0)))
    v_pad = jnp.pad(v, ((0, 0), (pad, 0), (0, 0), (0, 0)))
    q_chunks = jnp.moveaxis(q.reshape(bsz, nc, CHUNK, B_HEADS, B_HEAD_DIM), 1, 0)

    r = jnp.arange(CHUNK)
    m = jnp.arange(BAND)
    rel = (pad + r[:, None]) - m[None, :]
    idx = jnp.clip(rel, -REL_CLIP, REL_CLIP) + REL_CLIP
    bias = rel_bias.astype(jnp.float32)[:, idx]
    scale = B_HEAD_DIM ** -0.5

    def one_chunk(args):
        n, q_n = args
        start = n * CHUNK
        k_n = lax.dynamic_slice_in_dim(k_pad, start, BAND, axis=1)
        v_n = lax.dynamic_slice_in_dim(v_pad, start, BAND, axis=1)
        s = jnp.einsum('bchd,bmhd->bhcm', q_n, k_n).astype(jnp.float32) * scale + bias
        valid = (start - pad + m) >= 0
        s = jnp.where(valid[None, None, None, :], s, -jnp.inf)
        p = jax.nn.softmax(s, axis=-1).astype(v_n.dtype)
        return jnp.einsum('bhcm,bmhd->bchd', p, v_n)

    o = lax.map(one_chunk, (jnp.arange(nc, dtype=jnp.int32), q_chunks))
    o = jnp.moveaxis(o, 0, 1).reshape(bsz, t, B_INNER)
    o = (o.astype(jnp.float32) * jax.nn.silu(z.astype(jnp.float32))).astype(h.dtype)
    return o @ w_out.astype(h.dtype)


def setup_inputs(seed: int = 0) -> dict:
    key = jax.random.key(seed)
    ks = jax.random.split(key, 16)
    n_a = (DEPTH + 1) // N_MIXERS
    n_b = DEPTH // N_MIXERS
    f32 = jnp.float32
    x = jax.random.normal(ks[0], (BATCH, SEQ, D_MODEL), f32)
    norm_w = 1.0 + 0.02 * jax.random.normal(ks[1], (DEPTH, D_MODEL), f32)
    a_w_in = jax.random.normal(ks[2], (n_a, D_MODEL, 4 * A_INNER + 2 * A_HEADS), f32) * D_MODEL ** -0.5
    a_conv_w = jax.random.normal(ks[3], (n_a, CONV_K, 3 * A_INNER), f32) * CONV_K ** -0.5
    a_a_log = jnp.log(jax.random.uniform(ks[4], (n_a, A_HEADS), f32, 1.0, 16.0))
    dt = jnp.exp(jax.random.uniform(ks[5], (n_a, A_HEADS), f32, math.log(1e-3), math.log(1e-1)))
    a_dt_bias = dt + jnp.log(-jnp.expm1(-dt))
    a_out_norm_w = 1.0 + 0.02 * jax.random.normal(ks[6], (n_a, A_HEAD_DIM), f32)
    a_w_out = jax.random.normal(ks[7], (n_a, A_INNER, D_MODEL), f32) * A_INNER ** -0.5
    b_w_in = jax.random.normal(ks[8], (n_b, D_MODEL, 4 * B_INNER), f32) * D_MODEL ** -0.5
    b_q_norm_w = 1.0 + 0.02 * jax.random.normal(ks[9], (n_b, B_HEAD_DIM), f32)
    b_k_norm_w = 1.0 + 0.02 * jax.random.normal(ks[10], (n_b, B_HEAD_DIM), f32)
    b_rel_bias = 0.5 * jax.random.normal(ks[11], (n_b, B_HEADS, 2 * REL_CLIP + 1), f32)
    b_w_out = jax.random.normal(ks[12], (n_b, B_INNER, D_MODEL), f32) * B_INNER ** -0.5
    return {"x": x, "norm_w": norm_w, "a_w_in": a_w_in, "a_conv_w": a_conv_w,
            "a_a_log": a_a_log, "a_dt_bias": a_dt_bias, "a_out_norm_w": a_out_norm_w,
            "a_w_out": a_w_out, "b_w_in": b_w_in, "b_q_norm_w": b_q_norm_w,
            "b_k_norm_w": b_k_norm_w, "b_rel_bias": b_rel_bias, "b_w_out": b_w_out}


def reference(x, norm_w, a_w_in, a_conv_w, a_a_log, a_dt_bias, a_out_norm_w, a_w_out,
              b_w_in, b_q_norm_w, b_k_norm_w, b_rel_bias, b_w_out):
    h = x
    for i in range(DEPTH):
        j = i // N_MIXERS
        hn = rms_norm(h, norm_w[i])
        if i % N_MIXERS == 0:
            y = gated_deltanet(hn, a_w_in[j], a_conv_w[j], a_a_log[j], a_dt_bias[j],
                               a_out_norm_w[j], a_w_out[j])
        else:
            y = chunk_band_attention(hn, b_w_in[j], b_q_norm_w[j], b_k_norm_w[j],
                                     b_rel_bias[j], b_w_out[j])
        h = h + y
    return h
```

```python
import numpy as np
import ml_dtypes
import concourse.bass as bass
import concourse.mybir as mybir
from concourse.bass_utils import run_bass_kernel_spmd

F32 = mybir.dt.float32
BF16 = mybir.dt.bfloat16
AF = mybir.ActivationFunctionType
ALU = mybir.AluOpType
AX = mybir.AxisListType

NCORES = 8
D = 2048
T = 8192
B = 2
NTOK = B * T
TPC = NTOK // NCORES
EPS = 1e-6
KC = D // 128
HD = 128
SEM_LIMIT = 30000


class Res:
    __slots__ = ("name", "last_w", "readers", "dsem", "dcount", "excl")

    def __init__(self, name, excl=False):
        self.name = name
        self.excl = excl
        self.last_w = None
        self.readers = []
        self.dsem = None
        self.dcount = 0


class Op:
    __slots__ = ("eng", "fn", "deps", "inc", "pos", "dma", "semkey", "val", "dres")

    def __init__(self, eng, fn, dma):
        self.eng = eng
        self.fn = fn
        self.deps = []
        self.inc = False
        self.pos = -1
        self.dma = dma
        self.semkey = None
        self.val = 0
        self.dres = None


class Prog:
    ENGS = ("pe", "act", "dve", "pool", "sp")

    def __init__(self, nc):
        self.nc = nc
        self.ops = {e: [] for e in self.ENGS}
        self.known = {e: {} for e in self.ENGS}
        self.out_dmas = []

    def emit(self, eng, fn, reads=(), writes=(), dma_res=None):
        op = Op(eng, fn, dma_res is not None)
        op.pos = len(self.ops[eng])
        op.dres = dma_res
        deps = []
        xr = [r for r in reads if r.excl]
        if xr:
            writes = list(writes) + [r for r in xr if r not in writes]
            reads = [r for r in reads if not r.excl]
        for r in reads:
            if r.last_w is not None:
                deps.append(r.last_w)
        for r in writes:
            if r.last_w is not None:
                deps.append(r.last_w)
            deps.extend(r.readers)
        kn = self.known[eng]
        for d in deps:
            if d is op:
                continue
            if d.dma:
                key = ("d", id(d.dres))
                v = d.val
            else:
                if d.eng == "pe" and eng == "pe":
                    continue
                key = ("e", d.eng)
                v = d.pos + 1
            if kn.get(key, 0) >= v:
                continue
            kn[key] = v
            d.inc = True
            op.deps.append(d)
        if op.dma:
            dma_res.dcount += 16
            op.val = dma_res.dcount
        for r in reads:
            r.readers.append(op)
        for r in writes:
            r.last_w = op
            r.readers = []
        self.ops[eng].append(op)
        return op

    def pe(self, fn, reads=(), writes=()):
        return self.emit("pe", fn, reads, writes)

    def act(self, fn, reads=(), writes=()):
        return self.emit("act", fn, reads, writes)

    def dve(self, fn, reads=(), writes=()):
        return self.emit("dve", fn, reads, writes)

    def pool(self, fn, reads=(), writes=()):
        return self.emit("pool", fn, reads, writes)

    def dma(self, q, out, in_, reads, writes, sres, is_output=False):
        op = self.emit(q, lambda e: e.dma_start(out=out, in_=in_), reads, writes, dma_res=sres)
        if is_output:
            self.out_dmas.append(op)
        return op

    def finalize(self, stack):
        nc = self.nc
        tail_waits = []
        for d in self.out_dmas:
            d.inc = True
            tail_waits.append(d)
        sems = {}

        def get_sem(key):
            if key not in sems:
                sems[key] = stack.enter_context(nc.semaphore("s%d" % len(sems)))
            return sems[key]

        for e in self.ENGS:
            cnt = 0
            for op in self.ops[e]:
                if op.dma:
                    op.semkey = ("d", id(op.dres))
                    if op.inc or True:
                        get_sem(op.semkey)
                elif op.inc:
                    epoch = cnt // SEM_LIMIT
                    op.semkey = ("e", e, epoch)
                    op.val = cnt % SEM_LIMIT + 1
                    cnt += 1
                    get_sem(op.semkey)
        block = stack.enter_context(nc.Block())
        prog = self

        def run(engname):
            def body(eng):
                for op in prog.ops[engname]:
                    for d in op.deps:
                        eng.wait_ge(sems[d.semkey], d.val)
                    ins = op.fn(eng)
                    if op.dma:
                        ins.then_inc(sems[op.semkey], 16)
                    elif op.inc:
                        ins.then_inc(sems[op.semkey], 1)
                if engname == "sp":
                    for d in tail_waits:
                        eng.wait_ge(sems[d.semkey], d.val)
            return body

        block.tensor(run("pe"))
        block.scalar(run("act"))
        block.vector(run("dve"))
        block.gpsimd(run("pool"))
        block.sync(run("sp"))


class Ctx:
    def __init__(self, nc, stack):
        self.nc = nc
        self.stack = stack
        self.P = Prog(nc)
        self.n = 0

    def sb(self, shape, dt, name=None):
        self.n += 1
        nm = "%s_%d" % (name or "sb", self.n)
        t = self.stack.enter_context(self.nc.sbuf_tensor(nm, list(shape), dt))
        return t, Res(nm)

    def ps(self, shape, dt, name=None):
        self.n += 1
        nm = "%s_%d" % (name or "ps", self.n)
        t = self.stack.enter_context(self.nc.psum_tensor(nm, list(shape), dt))
        return t, Res(nm, excl=True)

    def dram(self, name, shape, dt, kind):
        return self.nc.dram_tensor(name, list(shape), dt, kind=kind).ap(), Res(name)


def make_ident(cx, dt):
    P = cx.P
    idf, r_idf = cx.sb([128, 128], F32, "identf")
    P.pool(lambda e: e.memset(idf[:], 1.0), writes=[r_idf])
    P.pool(lambda e: e.affine_select(out=idf[:], in_=idf[:], pattern=[[-1, 128]],
                                     compare_op=ALU.is_equal, fill=0.0, base=0,
                                     channel_multiplier=1), reads=[r_idf], writes=[r_idf])
    if dt == F32:
        return idf, r_idf
    idb, r_idb = cx.sb([128, 128], dt, "identb")
    P.dve(lambda e: e.tensor_copy(out=idb[:], in_=idf[:]), reads=[r_idf], writes=[r_idb])
    return idb, r_idb


def rmsnorm_transpose_tile(cx, src, r_src, nw_bc, r_nw, ident, r_ident, dstT, r_dstT, tcol,
                           junk, r_junk, st, r_st, hn, r_hn, pst, r_pst):
    P = cx.P
    P.act(lambda e: e.activation(out=junk[:], in_=src, func=AF.Square, accum_out=st[:, 0:1]),
          reads=[r_src], writes=[r_junk, r_st])
    P.dve(lambda e: e.tensor_scalar(out=st[:, 1:2], in0=st[:, 0:1], scalar1=1.0 / D, scalar2=EPS,
                                    op0=ALU.mult, op1=ALU.add), reads=[r_st], writes=[r_st])
    P.act(lambda e: e.activation(out=st[:, 3:4], in_=st[:, 1:2], func=AF.Ln),
          reads=[r_st], writes=[r_st])
    P.act(lambda e: e.activation(out=st[:, 2:3], in_=st[:, 3:4], func=AF.Exp, scale=-0.5),
          reads=[r_st], writes=[r_st])
    P.dve(lambda e: e.scalar_tensor_tensor(out=hn[:], in0=src, scalar=st[:, 2:3], in1=nw_bc[:],
                                           op0=ALU.mult, op1=ALU.mult),
          reads=[r_src, r_st, r_nw], writes=[r_hn])
    for half in range(2):
        pt, r_pt = pst[half], r_pst[half]
        for j in range(8):
            kc = half * 8 + j
            P.pe(lambda e, kc=kc, j=j, pt=pt: e.transpose(out=pt[:, j * 128:(j + 1) * 128],
                                                          in_=hn[:, kc * 128:(kc + 1) * 128],
                                                          identity=ident[:]),
                 reads=[r_hn, r_ident], writes=[r_pt])
        eng = P.act if half == 0 else P.dve
        if half == 0:
            P.act(lambda e, pt=pt, half=half: e.copy(
                out=dstT[:, half * 8:(half + 1) * 8, tcol:tcol + 128],
                in_=pt[:].rearrange("p (k t) -> p k t", k=8)), reads=[r_pt], writes=[r_dstT])
        else:
            P.dve(lambda e, pt=pt, half=half: e.tensor_copy(
                out=dstT[:, half * 8:(half + 1) * 8, tcol:tcol + 128],
                in_=pt[:].rearrange("p (k t) -> p k t", k=8)), reads=[r_pt], writes=[r_dstT])


def build_p0():
    from contextlib import ExitStack
    nc = bass.Bass("TRN2", target_bir_lowering=False)
    with ExitStack() as stack:
        cx = Ctx(nc, stack)
        P = cx.P
        x, r_x = cx.dram("x", [TPC, D], F32, "ExternalInput")
        nw, r_nwd = cx.dram("nw", [1, D], F32, "ExternalInput")
        hnT, r_hnT = cx.dram("hnT", [D, TPC], BF16, "ExternalOutput")
        ident, r_ident = make_ident(cx, BF16)
        nw_bc, r_nw = cx.sb([128, D], F32, "nwbc")
        P.dma("sp", nw_bc[:], nw[0:1, :].to_broadcast([128, D]), [r_nwd], [r_nw], r_nw)
        xt = [cx.sb([128, D], F32, "xt") for _ in range(2)]
        junk, r_junk = cx.sb([128, D], BF16, "junk")
        st, r_st = cx.sb([128, 4], F32, "st")
        hn, r_hn = cx.sb([128, D], BF16, "hn")
        pst = [cx.ps([128, 1024], BF16, "pst") for _ in range(2)]
        blk = [cx.sb([128, KC, 512], BF16, "blk") for _ in range(2)]
        hnT_v = hnT.rearrange("(k p) t -> p k t", p=128)
        for tb in range(TPC // 512):
            bt, r_bt = blk[tb % 2]
            for tt in range(4):
                ti = tb * 4 + tt
                xb, r_xb = xt[ti % 2]
                P.dma("sp", xb[:], x[ti * 128:(ti + 1) * 128, :], [r_x], [r_xb], r_xb)
                rmsnorm_transpose_tile(cx, xb[:], r_xb, nw_bc, r_nw, ident, r_ident, bt, r_bt,
                                       tt * 128, junk, r_junk, st, r_st, hn, r_hn,
                                       [p[0] for p in pst], [p[1] for p in pst])
            P.dma("pool", hnT_v[:, :, tb * 512:(tb + 1) * 512], bt[:], [r_bt], [r_hnT], r_bt,
                  is_output=True)
        P.finalize(stack)
    return nc


_CACHE = {}


def get_prog(name, builder):
    if name not in _CACHE:
        _CACHE[name] = builder()
    return _CACHE[name]


def run_spmd(nc, in_maps):
    res = run_bass_kernel_spmd(nc, in_maps, core_ids=list(range(NCORES)))
    return res.results


def phase0(x2d, nw):
    nc = get_prog("p0", build_p0)
    in_maps = [{"x": np.ascontiguousarray(x2d[c * TPC:(c + 1) * TPC]),
                "nw": np.ascontiguousarray(nw.reshape(1, D))} for c in range(NCORES)]
    res = run_spmd(nc, in_maps)
    return np.concatenate([r["hnT"] for r in res], axis=1)


def load_weight_bf16(cx, w_dram, r_wd, ncols, name, stg_list):
    P = cx.P
    wb, r_wb = cx.sb([128, KC, ncols], BF16, name)
    wv = w_dram.rearrange("(k p) n -> p k n", p=128)
    for kc in range(KC):
        stg, r_stg = stg_list[kc % len(stg_list)]
        P.dma("sp", stg[:, 0:ncols], wv[:, kc, :], [r_wd], [r_stg], r_stg)
        if kc % 2 == 0:
            P.pool(lambda e, kc=kc, stg=stg: e.tensor_copy(out=wb[:, kc, :], in_=stg[:, 0:ncols]),
                   reads=[r_stg], writes=[r_wb])
        else:
            P.dve(lambda e, kc=kc, stg=stg: e.tensor_copy(out=wb[:, kc, :], in_=stg[:, 0:ncols]),
                  reads=[r_stg], writes=[r_wb])
    return wb, r_wb


def build_outproj(with_norm):
    from contextlib import ExitStack
    nc = bass.Bass("TRN2", target_bir_lowering=False)
    with ExitStack() as stack:
        cx = Ctx(nc, stack)
        P = cx.P
        ogT, r_ogT = cx.dram("ogT", [D, TPC], BF16, "ExternalInput")
        w, r_w = cx.dram("w", [D, D], F32, "ExternalInput")
        xin, r_xin = cx.dram("xin", [TPC, D], F32, "ExternalInput")
        h, r_h = cx.dram("h", [TPC, D], F32, "ExternalOutput")
        ht = [cx.sb([128, D], F32, "ht") for _ in range(2)]
        xt = [cx.sb([128, D], F32, "xt") for _ in range(2)]
        wb, r_wb = load_weight_bf16(cx, w, r_w, D, "wout", ht)
        if with_norm:
            nw, r_nwd = cx.dram("nw", [1, D], F32, "ExternalInput")
            hnT, r_hnT = cx.dram("hnT", [D, TPC], BF16, "ExternalOutput")
            hnT_v = hnT.rearrange("(k p) t -> p k t", p=128)
            ident, r_ident = make_ident(cx, BF16)
            nw_bc, r_nw = cx.sb([128, D], F32, "nwbc")
            P.dma("sp", nw_bc[:], nw[0:1, :].to_broadcast([128, D]), [r_nwd], [r_nw], r_nw)
            junk, r_junk = cx.sb([128, D], BF16, "junk")
            st, r_st = cx.sb([128, 4], F32, "st")
            hn, r_hn = cx.sb([128, D], BF16, "hn")
            pst = [cx.ps([128, 1024], BF16, "pst") for _ in range(2)]
            blk = [cx.sb([128, KC, 512], BF16, "blk") for _ in range(2)]
        ogb = [cx.sb([128, KC, 512], BF16, "ogb") for _ in range(2)]
        pacc = [cx.ps([128, 512], F32, "pacc") for _ in range(4)]
        ogT_v = ogT.rearrange("(k p) t -> p k t", p=128)
        for tb in range(TPC // 512):
            og, r_og = ogb[tb % 2]
            P.dma("sp", og[:], ogT_v[:, :, tb * 512:(tb + 1) * 512], [r_ogT], [r_og], r_og)
            if with_norm:
                bt, r_bt = blk[tb % 2]
            for tt in range(4):
                ti = tb * 4 + tt
                xb, r_xb = xt[ti % 2]
                hb, r_hb = ht[ti % 2]
                P.dma("sp", xb[:], xin[ti * 128:(ti + 1) * 128, :], [r_xin], [r_xb], r_xb)
                for cb in range(4):
                    pa, r_pa = pacc[cb]
                    for kc in range(KC):
                        P.pe(lambda e, kc=kc, cb=cb, pa=pa, og=og, tt=tt: e.matmul(
                            pa[:], lhsT=og[:, kc, tt * 128:(tt + 1) * 128],
                            rhs=wb[:, kc, cb * 512:(cb + 1) * 512],
                            start=(kc == 0), stop=(kc == KC - 1)),
                            reads=[r_og, r_wb], writes=[r_pa])
                    P.dve(lambda e, cb=cb, pa=pa, hb=hb, xb=xb: e.tensor_tensor(
                        out=hb[:, cb * 512:(cb + 1) * 512], in0=pa[:],
                        in1=xb[:, cb * 512:(cb + 1) * 512], op=ALU.add),
                        reads=[r_pa, r_xb], writes=[r_hb])
                P.dma("pool", h[ti * 128:(ti + 1) * 128, :], hb[:], [r_hb], [r_h], r_hb,
                      is_output=True)
                if with_norm:
                    rmsnorm_transpose_tile(cx, hb[:], r_hb, nw_bc, r_nw, ident, r_ident, bt, r_bt,
                                           tt * 128, junk, r_junk, st, r_st, hn, r_hn,
                                           [p[0] for p in pst], [p[1] for p in pst])
            if with_norm:
                P.dma("pool", hnT_v[:, :, tb * 512:(tb + 1) * 512], bt[:], [r_bt], [r_hnT], r_bt,
                      is_output=True)
        P.finalize(stack)
    return nc


def phase_outproj(ogT_full, w, xin2d, nw=None):
    with_norm = nw is not None
    nc = get_prog("op%d" % with_norm, lambda: build_outproj(with_norm))
    in_maps = []
    for c in range(NCORES):
        m = {"ogT": np.ascontiguousarray(ogT_full[:, c * TPC:(c + 1) * TPC]),
             "w": np.ascontiguousarray(w),
             "xin": np.ascontiguousarray(xin2d[c * TPC:(c + 1) * TPC])}
        if with_norm:
            m["nw"] = np.ascontiguousarray(nw.reshape(1, D))
        in_maps.append(m)
    res = run_spmd(nc, in_maps)
    hh = np.concatenate([r["h"] for r in res], axis=0)
    if with_norm:
        return hh, np.concatenate([r["hnT"] for r in res], axis=1)
    return hh


NEG = -30000.0


import os
STAGE = int(os.environ.get('ATT_STAGE', '9'))


def build_attn(nblk_dbg=None):
    from contextlib import ExitStack
    nc = bass.Bass("TRN2", target_bir_lowering=False)
    NT = T // 128
    with ExitStack() as stack:
        cx = Ctx(nc, stack)
        P = cx.P
        hnT, r_hnT = cx.dram("hnT", [D, NTOK], BF16, "ExternalInput")
        wqk_d, r_wqk_d = cx.dram("wqk", [D, 512], F32, "ExternalInput")
        wvz_d, r_wvz_d = cx.dram("wvz", [D, 512], F32, "ExternalInput")
        qkw_d, r_qkw_d = cx.dram("qkw", [128, 2], F32, "ExternalInput")
        bias_d, r_bias_d = cx.dram("biasT", [128, 2 * 5 * 128], F32, "ExternalInput")
        ogT, r_ogT = cx.dram("ogT", [256, NTOK], BF16, "ExternalOutput")

        stg = [cx.sb([128, 512], F32, "stg") for _ in range(2)]
        wqk, r_wqk = load_weight_bf16(cx, wqk_d, r_wqk_d, 512, "wqk", stg)
        wvz, r_wvz = load_weight_bf16(cx, wvz_d, r_wvz_d, 512, "wvz", stg)
        ident, r_ident = make_ident(cx, BF16)
        qkw, r_qkw = cx.sb([128, 2], F32, "qkw")
        P.dma("sp", qkw[:], qkw_d[:, :], [r_qkw_d], [r_qkw], r_qkw)
        bias, r_bias = cx.sb([128, 2, 5, 128], F32, "bias")
        P.dma("sp", bias[:].rearrange("p a b c -> p (a b c)"), bias_d[:, :], [r_bias_d], [r_bias], r_bias)
        for hh in range(2):
            P.pool(lambda e, hh=hh: e.memset(bias[64:128, hh, 0, 0:64], NEG), writes=[r_bias])
            P.pool(lambda e, hh=hh: e.memset(bias[0:64, hh, 4, 64:128], NEG), writes=[r_bias])
        ones1, r_ones1 = cx.sb([128, 128], BF16, "ones1")
        onesk, r_onesk = cx.sb([128, 128], BF16, "onesk")
        P.pool(lambda e: e.memset(ones1[:], 1.0), writes=[r_ones1])
        P.pool(lambda e: e.memset(onesk[:], 1.0 / 128), writes=[r_onesk])
        c2, r_c2 = cx.sb([128, 2], F32, "c2")
        P.pool(lambda e: e.memset(c2[:, 0:1], 128 * EPS), writes=[r_c2])
        P.pool(lambda e: e.memset(c2[:, 1:2], EPS), writes=[r_c2])

        hb = [cx.sb([128, KC, 512], BF16, "hb") for _ in range(2)]
        kT, r_kT = cx.sb([128, 2, T], BF16, "kT")
        vall, r_vall = cx.sb([128, NT, 2, 132], BF16, "vall")
        P.pool(lambda e: e.memset(vall[:, :, :, 128:132], 1.0), writes=[r_vall])
        qn, r_qn = cx.sb([128, 2, 512], BF16, "qn")
        zs, r_zs = cx.sb([128, 4, 256], F32, "zs")
        sq = [cx.sb([128, 512], BF16, "sq") for _ in range(2)]
        lnv = [cx.sb([128, 512], F32, "lnv") for _ in range(2)]
        rinv = [cx.sb([128, 512], F32, "rinv") for _ in range(2)]
        Eb = [cx.sb([128, 640], F32, "E") for _ in range(2)]
        PT = [cx.sb([128, 640], BF16, "PT") for _ in range(2)]
        rs = [cx.sb([128, 1], F32, "rs") for _ in range(2)]
        og = [cx.sb([128, 128], BF16, "og") for _ in range(2)]
        ogb = [cx.sb([128, 2, 512], BF16, "ogb") for _ in range(2)]

        pq = [cx.ps([128, 512], F32, "pq") for _ in range(2)]
        pss, r_pss = cx.ps([128, 512], F32, "pss")
        pS = cx.ps([128, 1024], F32, "pS")
        pO = [cx.ps([128, 512], F32, "pO") for _ in range(2)]
        pT, r_pT = cx.ps([128, 1024], BF16, "pT")

        hnT_v = hnT.rearrange("(k p) t -> p k t", p=128)
        ogT_v = ogT.rearrange("(h p) t -> p h t", p=128)
        cnt = 0
        for b in range(B):
            for blk in range(T // 512):
                gblk = b * (T // 512) + blk
                if nblk_dbg is not None and gblk >= nblk_dbg:
                    continue
                h_, r_h = hb[gblk % 2]
                tok0 = b * T + blk * 512
                P.dma("sp", h_[:], hnT_v[:, :, tok0:tok0 + 512], [r_hnT], [r_h], r_h)
                for cc in range(4):
                    pa, r_pa = pq[cc % 2]
                    for kc in range(KC):
                        P.pe(lambda e, kc=kc, cc=cc, pa=pa, h_=h_: e.matmul(
                            pa[:], lhsT=wqk[:, kc, cc * 128:(cc + 1) * 128], rhs=h_[:, kc, :],
                            start=(kc == 0), stop=(kc == KC - 1)), reads=[r_wqk, r_h], writes=[r_pa])
                    s_, r_s = sq[cc % 2]
                    l_, r_l = lnv[cc % 2]
                    ri, r_ri = rinv[cc % 2]
                    isq = cc < 2
                    hh = cc % 2
                    P.act(lambda e, pa=pa, s_=s_: e.activation(out=s_[:], in_=pa[:], func=AF.Square),
                          reads=[r_pa], writes=[r_s])
                    om, r_om = (ones1, r_ones1) if isq else (onesk, r_onesk)
                    P.pe(lambda e, s_=s_, om=om: e.matmul(pss[:], lhsT=om[:], rhs=s_[:], start=True, stop=True),
                         reads=[r_s, r_om], writes=[r_pss])
                    ci = 0 if isq else 1
                    P.act(lambda e, l_=l_, ci=ci: e.activation(out=l_[:], in_=pss[:], func=AF.Ln,
                                                               bias=c2[:, ci:ci + 1]),
                          reads=[r_pss, r_c2], writes=[r_l])
                    P.act(lambda e, l_=l_, ri=ri: e.activation(out=ri[:], in_=l_[:], func=AF.Exp, scale=-0.5),
                          reads=[r_l], writes=[r_ri])
                    if isq:
                        P.dve(lambda e, pa=pa, ri=ri, hh=hh: e.scalar_tensor_tensor(
                            out=qn[:, hh, :], in0=pa[:], scalar=qkw[:, 0:1], in1=ri[:],
                            op0=ALU.mult, op1=ALU.mult), reads=[r_pa, r_qkw, r_ri], writes=[r_qn])
                    else:
                        P.dve(lambda e, pa=pa, ri=ri, hh=hh, blk=blk: e.scalar_tensor_tensor(
                            out=kT[:, hh, blk * 512:(blk + 1) * 512], in0=pa[:], scalar=qkw[:, 1:2],
                            in1=ri[:], op0=ALU.mult, op1=ALU.mult),
                            reads=[r_pa, r_qkw, r_ri], writes=[r_kT])
                if STAGE < 2:
                    continue
                for tt in range(4):
                    Q = blk * 4 + tt
                    pa, r_pa = pq[tt % 2]
                    for kc in range(KC):
                        P.pe(lambda e, kc=kc, tt=tt, pa=pa, h_=h_: e.matmul(
                            pa[:], lhsT=h_[:, kc, tt * 128:(tt + 1) * 128], rhs=wvz[:, kc, :],
                            start=(kc == 0), stop=(kc == KC - 1)), reads=[r_wvz, r_h], writes=[r_pa])
                    P.dve(lambda e, pa=pa, Q=Q: e.tensor_copy(
                        out=vall[:, Q, :, 0:128], in_=pa[:, 0:256].rearrange("p (h d) -> p h d", h=2)),
                        reads=[r_pa], writes=[r_vall])
                    P.act(lambda e, pa=pa, tt=tt: e.activation(out=zs[:, tt, :], in_=pa[:, 256:512],
                                                               func=AF.Silu), reads=[r_pa], writes=[r_zs])
                if STAGE < 3:
                    continue
                ob, r_ob = ogb[gblk % 2]
                for tt in range(4):
                    Q = blk * 4 + tt
                    nj = min(4, Q) + 1
                    for hh in range(2):
                        ps_, r_ps = pS
                        E_, r_E = Eb[cnt % 2]
                        PT_, r_PT = PT[cnt % 2]
                        po, r_po = pO[cnt % 2]
                        rs_, r_rs = rs[cnt % 2]
                        og_, r_og = og[cnt % 2]
                        for j in range(nj):
                            P.pe(lambda e, j=j, hh=hh, Q=Q, tt=tt, ps_=ps_: e.matmul(
                                ps_[:, j * 128:(j + 1) * 128],
                                lhsT=kT[:, hh, (Q - j) * 128:(Q - j + 1) * 128],
                                rhs=qn[:, hh, tt * 128:(tt + 1) * 128], start=True, stop=True),
                                reads=[r_kT, r_qn], writes=[r_ps])
                        P.dve(lambda e, nj=nj, hh=hh, ps_=ps_, E_=E_: e.tensor_tensor(
                            out=E_[:, 0:nj * 128], in0=ps_[:, 0:nj * 128],
                            in1=bias[:, hh, 0:nj, :].rearrange("p a b -> p (a b)"), op=ALU.add),
                            reads=[r_ps, r_bias], writes=[r_E])
                        P.act(lambda e, nj=nj, E_=E_, PT_=PT_: e.activation(
                            out=PT_[:, 0:nj * 128], in_=E_[:, 0:nj * 128], func=AF.Exp),
                            reads=[r_E], writes=[r_PT])
                        if STAGE < 4:
                            continue
                        for j in range(nj):
                            P.pe(lambda e, j=j, hh=hh, Q=Q, nj=nj, po=po, PT_=PT_: e.matmul(
                                po[:, 0:132], lhsT=PT_[:, j * 128:(j + 1) * 128],
                                rhs=vall[:, Q - j, hh, 0:132], start=(j == 0), stop=(j == nj - 1)),
                                reads=[r_PT, r_vall], writes=[r_po])
                        P.dve(lambda e, po=po, rs_=rs_: e.reciprocal(out=rs_[:], in_=po[:, 128:129]),
                              reads=[r_po], writes=[r_rs])
                        P.dve(lambda e, po=po, rs_=rs_, og_=og_, tt=tt, hh=hh: e.scalar_tensor_tensor(
                            out=og_[:], in0=po[:, 0:128], scalar=rs_[:, 0:1],
                            in1=zs[:, tt, hh * 128:(hh + 1) * 128], op0=ALU.mult, op1=ALU.mult),
                            reads=[r_po, r_rs, r_zs], writes=[r_og])
                        slot = cnt % 8
                        P.pe(lambda e, og_=og_, slot=slot: e.transpose(
                            out=pT[:, slot * 128:(slot + 1) * 128], in_=og_[:], identity=ident[:]),
                            reads=[r_og, r_ident], writes=[r_pT])
                        P.act(lambda e, slot=slot, ob=ob, hh=hh, tt=tt: e.copy(
                            out=ob[:, hh, tt * 128:(tt + 1) * 128], in_=pT[:, slot * 128:(slot + 1) * 128]),
                            reads=[r_pT], writes=[r_ob])
                        cnt += 1
                P.dma("pool", ogT_v[:, :, tok0:tok0 + 512], ob[:], [r_ob], [r_ogT], r_ob, is_output=True)
        P.finalize(stack)
    return nc


def head_cols(c, base, width=128):
    return np.concatenate([np.arange(base + (2 * c + i) * width, base + (2 * c + i + 1) * width)
                           for i in range(2)])


def phase_attn(hnT_full, w_in, q_norm_w, k_norm_w, rel_bias, nblk_dbg=None):
    nc = get_prog("attn", lambda: build_attn(nblk_dbg))
    m = np.arange(128)[:, None]
    r = np.arange(128)[None, :]
    idx = np.stack([np.clip(j * 128 + r - m, -256, 256) + 256 for j in range(5)], 0)
    qkw = np.ascontiguousarray(np.stack([q_norm_w, k_norm_w], axis=1).astype(np.float32))
    in_maps = []
    for c in range(NCORES):
        wqk = w_in[:, np.concatenate([head_cols(c, 0), head_cols(c, 2048)])]
        wvz = w_in[:, np.concatenate([head_cols(c, 4096), head_cols(c, 6144)])]
        bt = rel_bias[2 * c:2 * c + 2][:, idx]
        bt = np.ascontiguousarray(np.transpose(bt, (2, 0, 1, 3))).reshape(128, 2 * 5 * 128)
        in_maps.append({"hnT": hnT_full, "wqk": np.ascontiguousarray(wqk),
                        "wvz": np.ascontiguousarray(wvz), "qkw": qkw,
                        "biasT": np.ascontiguousarray(bt)})
    res = run_spmd(nc, in_maps)
    return np.concatenate([r_["ogT"] for r_ in res], axis=0)


BIG = 30000.0


def build_gdn(nblk_dbg=None):
    from contextlib import ExitStack
    nc = bass.Bass("TRN2", target_bir_lowering=False)
    with ExitStack() as stack:
        cx = Ctx(nc, stack)
        P = cx.P
        hnT, r_hnT = cx.dram("hnT", [D, NTOK], BF16, "ExternalInput")
        wqkv_d, r_wqkv_d = cx.dram("wqkv", [D, 768], F32, "ExternalInput")
        wzg_d, r_wzg_d = cx.dram("wzg", [D, 260], F32, "ExternalInput")
        cw_d, r_cw_d = cx.dram("cw", [128, 24], F32, "ExternalInput")
        gp_d, r_gp_d = cx.dram("gp", [1, 4], F32, "ExternalInput")
        onw_d, r_onw_d = cx.dram("onw", [1, 256], F32, "ExternalInput")
        ogT, r_ogT = cx.dram("ogT", [256, NTOK], BF16, "ExternalOutput")

        stg = [cx.sb([128, 768], F32, "stg") for _ in range(2)]
        wqkv, r_wqkv = load_weight_bf16(cx, wqkv_d, r_wqkv_d, 768, "wqkv", stg)
        wzg, r_wzg = load_weight_bf16(cx, wzg_d, r_wzg_d, 260, "wzg", stg)
        identb, r_identb = make_ident(cx, BF16)
        identf, r_identf = make_ident(cx, F32)
        cw, r_cw = cx.sb([128, 24], F32, "cw")
        P.dma("sp", cw[:], cw_d[:, :], [r_cw_d], [r_cw], r_cw)
        gp, r_gp = cx.sb([128, 4], F32, "gp")
        P.dma("sp", gp[:], gp_d[0:1, :].to_broadcast([128, 4]), [r_gp_d], [r_gp], r_gp)
        onw, r_onw = cx.sb([128, 256], F32, "onw")
        P.dma("sp", onw[:], onw_d[0:1, :].to_broadcast([128, 256]), [r_onw_d], [r_onw], r_onw)
        negA, r_negA = cx.sb([128, 2], F32, "negA")
        P.act(lambda e: e.activation(out=negA[:], in_=gp[:, 0:2], func=AF.Exp), reads=[r_gp], writes=[r_negA])
        P.dve(lambda e: e.tensor_scalar(out=negA[:], in0=negA[:], scalar1=-1.0, scalar2=None, op0=ALU.mult),
              reads=[r_negA], writes=[r_negA])

        def const(name, dt=F32):
            return cx.sb([128, 128], dt, name)

        onesf, r_onesf = const("onesf")
        nonesf, r_nonesf = const("nonesf")
        P.pool(lambda e: e.memset(onesf[:], 1.0), writes=[r_onesf])
        P.pool(lambda e: e.memset(nonesf[:], -1.0), writes=[r_nonesf])
        ones1, r_ones1 = const("ones1", BF16)
        ones128, r_ones128 = const("ones128", BF16)
        P.pool(lambda e: e.memset(ones1[:], 1.0), writes=[r_ones1])
        P.pool(lambda e: e.memset(ones128[:], 128.0), writes=[r_ones128])
        mA, r_mA = const("mA")
        mB, r_mB = const("mB")
        tri, r_tri = const("tri")
        blkm, r_blkm = const("blkm")
        P.pool(lambda e: e.memset(mA[:], 0.0), writes=[r_mA])
        P.pool(lambda e: e.affine_select(out=mA[:], in_=mA[:], pattern=[[-1, 128]], compare_op=ALU.is_gt,
                                         fill=-BIG, base=0, channel_multiplier=1), reads=[r_mA], writes=[r_mA])
        P.pool(lambda e: e.memset(mA[64:128, 0:64], -BIG), writes=[r_mA])
        P.pool(lambda e: e.memset(mB[:], 0.0), writes=[r_mB])
        P.pool(lambda e: e.affine_select(out=mB[:], in_=mB[:], pattern=[[1, 128]], compare_op=ALU.is_ge,
                                         fill=-BIG, base=0, channel_multiplier=-1), reads=[r_mB], writes=[r_mB])
        P.pool(lambda e: e.memset(mB[0:64, 64:128], -BIG), writes=[r_mB])
        P.pool(lambda e: e.memset(tri[:], 1.0), writes=[r_tri])
        P.pool(lambda e: e.affine_select(out=tri[:], in_=tri[:], pattern=[[1, 128]], compare_op=ALU.is_ge,
                                         fill=0.0, base=0, channel_multiplier=-1), reads=[r_tri], writes=[r_tri])
        P.pool(lambda e: e.memset(tri[0:64, 64:128], 0.0), writes=[r_tri])
        P.pool(lambda e: e.memset(blkm[:], 0.0), writes=[r_blkm])
        P.pool(lambda e: e.memset(blkm[0:64, 0:64], 1.0), writes=[r_blkm])
        P.pool(lambda e: e.memset(blkm[64:128, 64:128], 1.0), writes=[r_blkm])
        cst, r_cst = cx.sb([128, 8], F32, "cst")
        P.pool(lambda e: e.memset(cst[:, 0:1], 1.0), writes=[r_cst])
        P.pool(lambda e: e.memset(cst[:, 1:2], EPS), writes=[r_cst])
        P.pool(lambda e: e.memset(cst[:, 2:3], 128 * EPS), writes=[r_cst])
        P.pool(lambda e: e.memset(cst[:, 3:5], 0.0), writes=[r_cst])
        P.pool(lambda e: e.memset(cst[0:64, 3:4], 1.0), writes=[r_cst])
        P.pool(lambda e: e.memset(cst[64:128, 4:5], 1.0), writes=[r_cst])
        P.pool(lambda e: e.memset(cst[:, 5:6], EPS), writes=[r_cst])

        hb = [cx.sb([128, KC, 512], BF16, "hb") for _ in range(2)]
        pre = [cx.sb([128, 516], F32, "pre") for _ in range(6)]
        ybuf = [cx.sb([128, 512], F32, "y") for _ in range(2)]
        xs = [cx.sb([128, 512], F32, "xs") for _ in range(4)]
        vT = [cx.sb([128, 512], BF16, "vT") for _ in range(2)]
        sq = [cx.sb([128, 512], BF16, "sq") for _ in range(2)]
        lnv = [cx.sb([128, 512], F32, "lnv") for _ in range(2)]
        rinv = [cx.sb([128, 512], F32, "rinv") for _ in range(2)]
        qn, r_qn = cx.sb([128, 2, 512], BF16, "qn")
        kn, r_kn = cx.sb([128, 2, 512], BF16, "kn")
        zw, r_zw = cx.sb([128, 4, 256], F32, "zw")
        ab, r_ab = cx.sb([128, 4, 4], F32, "ab")
        gt, r_gt = cx.sb([128, 32], F32, "gt")
        gl, r_gl = cx.sb([128, 4], F32, "gl")
        qz, r_qz = cx.sb([128, 4, 64], BF16, "qz")
        P.pool(lambda e: e.memset(qz[:], 0.0), writes=[r_qz])
        f32t = {n: cx.sb([128, 128], F32, n) for n in ("diag", "diag2", "decb", "decT", "egcm", "u")}
        bft = {n: cx.sb([128, 128], BF16, n) for n in
               ("kbg", "kdec", "vb", "L", "M", "P0", "P1", "Q0", "Q1", "R0", "R1", "qkT", "wT", "vn", "og", "junk")}
        Sf = [cx.sb([128, 128], F32, "Sf") for _ in range(2)]
        Sb = [[cx.sb([128, 128], BF16, "Sb") for _ in range(2)] for _ in range(2)]
        st, r_st = cx.sb([128, 4], F32, "st")
        ogb = [cx.sb([128, 2, 512], BF16, "ogb") for _ in range(2)]

        pq = [cx.ps([128, 512], F32, "pq") for _ in range(2)]
        pss, r_pss = cx.ps([128, 512], F32, "pss")
        pt, r_pt = cx.ps([128, 1024], BF16, "pt")
        pA, r_pA = cx.ps([128, 512], F32, "pA")
        pK, r_pK = cx.ps([128, 512], F32, "pK")
        pR, r_pR = cx.ps([128, 512], F32, "pR")
        pscan, r_pscan = cx.ps([128, 512], F32, "pscan")

        hnT_v = hnT.rearrange("(k p) t -> p k t", p=128)
        ogT_v = ogT.rearrange("(h p) t -> p h t", p=128)
        scur = [0, 0]
        for b in range(B):
            for hh in range(2):
                sf, r_sf = Sf[hh]
                P.pool(lambda e, sf=sf: e.memset(sf[:], 0.0), writes=[r_sf])
                s0, r_s0 = Sb[hh][scur[hh]]
                P.pool(lambda e, s0=s0: e.memset(s0[:], 0.0), writes=[r_s0])
            for blk in range(T // 512):
                gblk = b * (T // 512) + blk
                if nblk_dbg is not None and gblk >= nblk_dbg:
                    continue
                h_, r_h = hb[gblk % 2]
                tok0 = b * T + blk * 512
                P.dma("sp", h_[:], hnT_v[:, :, tok0:tok0 + 512], [r_hnT], [r_h], r_h)
                for cc in range(6):
                    pa, r_pa = pq[cc % 2]
                    pr, r_pr = pre[cc]
                    for kc in range(KC):
                        P.pe(lambda e, kc=kc, cc=cc, pa=pa, h_=h_: e.matmul(
                            pa[:], lhsT=wqkv[:, kc, cc * 128:(cc + 1) * 128], rhs=h_[:, kc, :],
                            start=(kc == 0), stop=(kc == KC - 1)), reads=[r_wqkv, r_h], writes=[r_pa])
                    if blk == 0:
                        P.pool(lambda e, pr=pr: e.memset(pr[:, 0:4], 0.0), writes=[r_pr])
                    else:
                        P.act(lambda e, pr=pr: e.copy(out=pr[:, 0:4], in_=pr[:, 512:516]),
                              reads=[r_pr], writes=[r_pr])
                    P.act(lambda e, pr=pr, pa=pa: e.copy(out=pr[:, 4:516], in_=pa[:]), reads=[r_pa], writes=[r_pr])
                    y, r_y = ybuf[cc % 2]
                    P.dve(lambda e, y=y, pr=pr, cc=cc: e.tensor_scalar(
                        out=y[:], in0=pr[:, 1:513], scalar1=cw[:, cc * 4:cc * 4 + 1], scalar2=None, op0=ALU.mult),
                        reads=[r_pr, r_cw], writes=[r_y])
                    for j in range(1, 4):
                        P.dve(lambda e, y=y, pr=pr, cc=cc, j=j: e.scalar_tensor_tensor(
                            out=y[:], in0=pr[:, 1 + j:513 + j], scalar=cw[:, cc * 4 + j:cc * 4 + j + 1], in1=y[:],
                            op0=ALU.mult, op1=ALU.add), reads=[r_pr, r_cw, r_y], writes=[r_y])
                    if cc < 4:
                        x_, r_x = xs[cc]
                        P.act(lambda e, y=y, x_=x_: e.activation(out=x_[:], in_=y[:], func=AF.Silu),
                              reads=[r_y], writes=[r_x])
                    else:
                        v_, r_v = vT[cc - 4]
                        P.act(lambda e, y=y, v_=v_: e.activation(out=v_[:], in_=y[:], func=AF.Silu),
                              reads=[r_y], writes=[r_v])
                for cc in range(4):
                    x_, r_x = xs[cc]
                    s_, r_s = sq[cc % 2]
                    l_, r_l = lnv[cc % 2]
                    ri, r_ri = rinv[cc % 2]
                    isq = cc < 2
                    hh = cc % 2
                    P.act(lambda e, x_=x_, s_=s_: e.activation(out=s_[:], in_=x_[:], func=AF.Square),
                          reads=[r_x], writes=[r_s])
                    om, r_om = (ones128, r_ones128) if isq else (ones1, r_ones1)
                    P.pe(lambda e, s_=s_, om=om: e.matmul(pss[:], lhsT=om[:], rhs=s_[:], start=True, stop=True),
                         reads=[r_s, r_om], writes=[r_pss])
                    ci = 2 if isq else 1
                    P.act(lambda e, l_=l_, ci=ci: e.activation(out=l_[:], in_=pss[:], func=AF.Ln,
                                                               bias=cst[:, ci:ci + 1]),
                          reads=[r_pss, r_cst], writes=[r_l])
                    P.act(lambda e, l_=l_, ri=ri: e.activation(out=ri[:], in_=l_[:], func=AF.Exp, scale=-0.5),
                          reads=[r_l], writes=[r_ri])
                    dst, r_dst = (qn, r_qn) if isq else (kn, r_kn)
                    P.pool(lambda e, x_=x_, ri=ri, dst=dst, hh=hh: e.tensor_tensor(
                        out=dst[:, hh, :], in0=x_[:], in1=ri[:], op=ALU.mult),
                        reads=[r_x, r_ri], writes=[r_dst])
                ob, r_ob = ogb[gblk % 2]
                for tt in range(4):
                    c0 = tt * 128
                    pa, r_pa = pq[tt % 2]
                    for kc in range(KC):
                        P.pe(lambda e, kc=kc, c0=c0, pa=pa, h_=h_: e.matmul(
                            pa[:, 0:260], lhsT=h_[:, kc, c0:c0 + 128], rhs=wzg[:, kc, :],
                            start=(kc == 0), stop=(kc == KC - 1)), reads=[r_wzg, r_h], writes=[r_pa])
                    P.act(lambda e, pa=pa, tt=tt: e.activation(out=zw[:, tt, :], in_=pa[:, 0:256], func=AF.Silu),
                          reads=[r_pa], writes=[r_zw])
                    P.dve(lambda e, pa=pa, tt=tt: e.tensor_copy(out=ab[:, tt, :], in_=pa[:, 256:260]),
                          reads=[r_pa], writes=[r_ab])
                    P.pool(lambda e, tt=tt: e.tensor_tensor(out=zw[:, tt, :], in0=zw[:, tt, :], in1=onw[:], op=ALU.mult),
                           reads=[r_zw, r_onw], writes=[r_zw])
                    G = [r_gt]
                    P.dve(lambda e, tt=tt: e.tensor_tensor(out=gt[:, 0:2], in0=ab[:, tt, 0:2], in1=gp[:, 2:4], op=ALU.add),
                          reads=[r_ab, r_gp], writes=G)
                    P.act(lambda e: e.activation(out=gt[:, 0:2], in_=gt[:, 0:2], func=AF.Exp), reads=G, writes=G)
                    P.act(lambda e: e.activation(out=gt[:, 0:2], in_=gt[:, 0:2], func=AF.Ln, bias=cst[:, 0:1]),
                          reads=G + [r_cst], writes=G)
                    P.dve(lambda e: e.tensor_tensor(out=gt[:, 2:4], in0=gt[:, 0:2], in1=negA[:], op=ALU.mult),
                          reads=G + [r_negA], writes=G)
                    P.act(lambda e, tt=tt: e.activation(out=gt[:, 4:6], in_=ab[:, tt, 2:4], func=AF.Exp, scale=-1.0),
                          reads=[r_ab], writes=G)
                    P.act(lambda e: e.activation(out=gt[:, 4:6], in_=gt[:, 4:6], func=AF.Ln, bias=cst[:, 0:1]),
                          reads=G + [r_cst], writes=G)
                    P.act(lambda e: e.activation(out=gt[:, 6:8], in_=gt[:, 4:6], func=AF.Exp, scale=-1.0),
                          reads=G, writes=G)
                    P.pe(lambda e: e.matmul(pA[:, 384:386], lhsT=tri[:], rhs=gt[:, 2:4], start=True, stop=True),
                         reads=G + [r_tri], writes=[r_pA])
                    P.pe(lambda e: e.matmul(pA[:, 388:390], lhsT=blkm[:], rhs=gt[:, 2:4], start=True, stop=True),
                         reads=G + [r_blkm], writes=[r_pA])
                    P.dve(lambda e: e.tensor_copy(out=gt[:, 8:10], in_=pA[:, 384:386]), reads=[r_pA], writes=G)
                    P.dve(lambda e: e.tensor_tensor(out=gt[:, 12:14], in0=pA[:, 388:390], in1=gt[:, 8:10], op=ALU.subtract),
                          reads=[r_pA] + G, writes=G)
                    P.dve(lambda e: e.tensor_tensor(out=gt[:, 10:12], in0=gt[:, 8:10], in1=gt[:, 4:6], op=ALU.subtract),
                          reads=G, writes=G)
                    P.act(lambda e: e.activation(out=gt[:, 12:14], in_=gt[:, 12:14], func=AF.Exp), reads=G, writes=G)
                    P.act(lambda e: e.activation(out=gt[:, 14:16], in_=gt[:, 8:10], func=AF.Exp), reads=G, writes=G)
                    P.dve(lambda e: e.tensor_tensor(out=gt[:, 16:18], in0=gt[:, 6:8], in1=gt[:, 14:16], op=ALU.mult),
                          reads=G, writes=G)
                    P.dve(lambda e: e.tensor_scalar(out=gt[:, 18:20], in0=gt[:, 2:4], scalar1=cst[:, 3:4], scalar2=None,
                                                    op0=ALU.mult), reads=G + [r_cst], writes=G)
                    P.dve(lambda e: e.tensor_scalar(out=gt[:, 20:22], in0=gt[:, 2:4], scalar1=cst[:, 4:5], scalar2=None,
                                                    op0=ALU.mult), reads=G + [r_cst], writes=G)
                    P.pe(lambda e: e.matmul(pA[:, 392:396], lhsT=onesf[:], rhs=gt[:, 18:22], start=True, stop=True),
                         reads=G + [r_onesf], writes=[r_pA])
                    P.act(lambda e: e.activation(out=gl[:], in_=pA[:, 392:396], func=AF.Exp), reads=[r_pA], writes=[r_gl])
                    for hh in range(2):
                        T_ = {n: f32t[n] for n in f32t}
                        Bt = {n: bft[n] for n in bft}
                        kn_t = kn[:, hh, c0:c0 + 128]
                        qn_t = qn[:, hh, c0:c0 + 128]
                        v_, r_v = vT[hh]

                        def sc(col):
                            return gt[:, col + hh:col + hh + 1]
                        P.pe(lambda e, kn_t=kn_t: e.transpose(out=pt[:, 0:128], in_=kn_t, identity=identb[:]),
                             reads=[r_kn, r_identb], writes=[r_pt])
                        P.pe(lambda e, v_=v_, c0=c0: e.transpose(out=pt[:, 128:256], in_=v_[:, c0:c0 + 128], identity=identb[:]),
                             reads=[r_v, r_identb], writes=[r_pt])
                        kbg, r_kbg = Bt["kbg"]
                        kdec, r_kdec = Bt["kdec"]
                        vb, r_vb = Bt["vb"]
                        P.dve(lambda e, kbg=kbg, s=sc(16): e.tensor_scalar(out=kbg[:], in0=pt[:, 0:128], scalar1=s, scalar2=None,
                                                                          op0=ALU.mult), reads=[r_pt] + G, writes=[r_kbg])
                        P.dve(lambda e, kdec=kdec, s=sc(12): e.tensor_scalar(out=kdec[:], in0=pt[:, 0:128], scalar1=s, scalar2=None,
                                                                            op0=ALU.mult), reads=[r_pt] + G, writes=[r_kdec])
                        P.dve(lambda e, vb=vb, s=sc(6): e.tensor_scalar(out=vb[:], in0=pt[:, 128:256], scalar1=s, scalar2=None,
                                                                       op0=ALU.mult), reads=[r_pt] + G, writes=[r_vb])
                        diag, r_diag = T_["diag"]
                        diag2, r_diag2 = T_["diag2"]
                        P.dve(lambda e, diag=diag, s=sc(8): e.tensor_scalar(out=diag[:], in0=identf[:], scalar1=s, scalar2=None,
                                                                           op0=ALU.mult), reads=[r_identf] + G, writes=[r_diag])
                        P.dve(lambda e, diag2=diag2, s=sc(10): e.tensor_scalar(out=diag2[:], in0=identf[:], scalar1=s, scalar2=None,
                                                                              op0=ALU.mult), reads=[r_identf] + G, writes=[r_diag2])
                        P.pe(lambda e, diag2=diag2: e.matmul(pA[:, 0:128], lhsT=diag2[:], rhs=onesf[:], start=True, stop=False),
                             reads=[r_diag2, r_onesf], writes=[r_pA])
                        P.pe(lambda e, diag=diag: e.matmul(pA[:, 0:128], lhsT=nonesf[:], rhs=diag[:], start=False, stop=False),
                             reads=[r_diag, r_nonesf], writes=[r_pA])
                        P.pe(lambda e: e.matmul(pA[:, 0:128], lhsT=identf[:], rhs=mA[:], start=False, stop=True),
                             reads=[r_identf, r_mA], writes=[r_pA])
                        P.pe(lambda e, diag=diag: e.matmul(pA[:, 128:256], lhsT=onesf[:], rhs=diag[:], start=True, stop=False),
                             reads=[r_diag, r_onesf], writes=[r_pA])
                        P.pe(lambda e, diag=diag: e.matmul(pA[:, 128:256], lhsT=diag[:], rhs=nonesf[:], start=False, stop=False),
                             reads=[r_diag, r_nonesf], writes=[r_pA])
                        P.pe(lambda e: e.matmul(pA[:, 128:256], lhsT=identf[:], rhs=mB[:], start=False, stop=True),
                             reads=[r_identf, r_mB], writes=[r_pA])
                        P.pe(lambda e, diag=diag: e.matmul(pA[:, 256:384], lhsT=onesf[:], rhs=diag[:], start=True, stop=True),
                             reads=[r_diag, r_onesf], writes=[r_pA])
                        decb, r_decb = T_["decb"]
                        decT, r_decT = T_["decT"]
                        egcm, r_egcm = T_["egcm"]
                        P.act(lambda e, decb=decb: e.activation(out=decb[:], in_=pA[:, 0:128], func=AF.Exp),
                              reads=[r_pA], writes=[r_decb])
                        P.act(lambda e, decT=decT: e.activation(out=decT[:], in_=pA[:, 128:256], func=AF.Exp),
                              reads=[r_pA], writes=[r_decT])
                        P.act(lambda e, egcm=egcm: e.activation(out=egcm[:], in_=pA[:, 256:384], func=AF.Exp),
                              reads=[r_pA], writes=[r_egcm])
                        P.pe(lambda e, kn_t=kn_t: e.matmul(pK[:, 0:128], lhsT=kn_t, rhs=kn_t, start=True, stop=True),
                             reads=[r_kn], writes=[r_pK])
                        P.pe(lambda e, kn_t=kn_t, qn_t=qn_t: e.matmul(pK[:, 128:256], lhsT=kn_t, rhs=qn_t, start=True, stop=True),
                             reads=[r_kn, r_qn], writes=[r_pK])
                        L, r_L = Bt["L"]
                        qkT, r_qkT = Bt["qkT"]
                        P.dve(lambda e, L=L, decb=decb: e.tensor_tensor(out=L[:], in0=pK[:, 0:128], in1=decb[:], op=ALU.mult),
                              reads=[r_pK, r_decb], writes=[r_L])
                        P.dve(lambda e, qkT=qkT, decT=decT: e.tensor_tensor(out=qkT[:], in0=pK[:, 128:256], in1=decT[:], op=ALU.mult),
                              reads=[r_pK, r_decT], writes=[r_qkT])
                        P.pool(lambda e, qn_t=qn_t, egcm=egcm: e.tensor_tensor(
                            out=qz[:, 0::3, :], in0=qn_t.rearrange("p (a b) -> p a b", a=2),
                            in1=egcm[:].rearrange("p (a b) -> p a b", a=2), op=ALU.mult),
                            reads=[r_qn, r_egcm], writes=[r_qz])
                        M_, r_M = Bt["M"]
                        R_, r_R = Bt["R0"]
                        P.pe(lambda e, L=L: e.transpose(out=pt[:, 256:384], in_=L[:], identity=identb[:]),
                             reads=[r_L, r_identb], writes=[r_pt])
                        P.dve(lambda e, M_=M_: e.tensor_copy(out=M_[:], in_=pt[:, 256:384]), reads=[r_pt], writes=[r_M])
                        P.dve(lambda e, R_=R_: e.tensor_tensor(out=R_[:], in0=identb[:], in1=pt[:, 256:384], op=ALU.subtract),
                              reads=[r_pt, r_identb], writes=[r_R])
                        Pm, r_Pm = L, r_L
                        Qm, r_Qm = M_, r_M
                        for j in range(1, 6):
                            Pn, r_Pn = Bt["P%d" % (j % 2)]
                            Qn, r_Qn = Bt["Q%d" % (j % 2)]
                            Rn, r_Rn = Bt["R%d" % (j % 2)]
                            P.pe(lambda e, Pm=Pm, Qm=Qm: e.matmul(pK[:, 256:384], lhsT=Qm[:], rhs=Pm[:], start=True, stop=True),
                                 reads=[r_Pm, r_Qm], writes=[r_pK])
                            if j < 5:
                                P.pe(lambda e, Pm=Pm, Qm=Qm: e.matmul(pK[:, 384:512], lhsT=Pm[:], rhs=Qm[:], start=True, stop=True),
                                     reads=[r_Pm, r_Qm], writes=[r_pK])
                            P.act(lambda e, Pn=Pn: e.copy(out=Pn[:], in_=pK[:, 256:384]), reads=[r_pK], writes=[r_Pn])
                            if j < 5:
                                P.dve(lambda e, Qn=Qn: e.tensor_copy(out=Qn[:], in_=pK[:, 384:512]), reads=[r_pK], writes=[r_Qn])
                            P.pe(lambda e, R_=R_: e.matmul(pR[:, 0:128], lhsT=identb[:], rhs=R_[:], start=True, stop=False),
                                 reads=[r_R, r_identb], writes=[r_pR])
                            P.pe(lambda e, R_=R_, Pn=Pn: e.matmul(pR[:, 0:128], lhsT=Pn[:], rhs=R_[:], start=False, stop=True),
                                 reads=[r_R, r_Pn], writes=[r_pR])
                            P.dve(lambda e, Rn=Rn: e.tensor_copy(out=Rn[:], in_=pR[:, 0:128]), reads=[r_pR], writes=[r_Rn])
                            Pm, r_Pm, Qm, r_Qm, R_, r_R = Pn, r_Pn, Qn, r_Qn, Rn, r_Rn
                        u, r_u = T_["u"]
                        wT, r_wT = Bt["wT"]
                        P.pe(lambda e, R_=R_, vb=vb: e.matmul(pR[:, 128:256], lhsT=R_[:], rhs=vb[:], start=True, stop=True),
                             reads=[r_R, r_vb], writes=[r_pR])
                        P.pe(lambda e, R_=R_, kbg=kbg: e.matmul(pR[:, 256:384], lhsT=kbg[:], rhs=R_[:], start=True, stop=True),
                             reads=[r_R, r_kbg], writes=[r_pR])
                        P.dve(lambda e, u=u: e.tensor_copy(out=u[:], in_=pR[:, 128:256]), reads=[r_pR], writes=[r_u])
                        P.act(lambda e, wT=wT: e.copy(out=wT[:], in_=pR[:, 256:384]), reads=[r_pR], writes=[r_wT])
                        vn, r_vn = Bt["vn"]
                        sf, r_sf = Sf[hh]
                        for j in range(2):
                            sb_, r_sb = Sb[hh][scur[hh]]
                            sn_, r_sn = Sb[hh][1 - scur[hh]]
                            lo, hi = j * 64, (j + 1) * 64
                            P.pe(lambda e, wT=wT, sb_=sb_: e.matmul(pscan[:, 0:128], lhsT=wT[:], rhs=sb_[:], start=True, stop=True),
                                 reads=[r_wT, r_sb], writes=[r_pscan])
                            P.pe(lambda e, j=j, sb_=sb_: e.matmul(
                                pR[:, 384:512], lhsT=qz[:, 2 * j:2 * j + 2, :].rearrange("p a b -> p (a b)"), rhs=sb_[:],
                                start=(j == 0), stop=False), reads=[r_qz, r_sb], writes=[r_pR])
                            P.dve(lambda e, lo=lo, hi=hi, vn=vn, u=u: e.tensor_tensor(
                                out=vn[lo:hi, :], in0=u[lo:hi, :], in1=pscan[lo:hi, 0:128], op=ALU.subtract),
                                reads=[r_u, r_pscan], writes=[r_vn])
                            P.pe(lambda e, lo=lo, hi=hi, vn=vn, kdec=kdec: e.matmul(
                                pscan[:, 128:256], lhsT=kdec[lo:hi, :], rhs=vn[lo:hi, :], start=True, stop=True),
                                reads=[r_kdec, r_vn], writes=[r_pscan])
                            P.dve(lambda e, sf=sf, j=j, hh=hh: e.scalar_tensor_tensor(
                                out=sf[:], in0=sf[:], scalar=gl[:, 2 * j + hh:2 * j + hh + 1], in1=pscan[:, 128:256],
                                op0=ALU.mult, op1=ALU.add), reads=[r_sf, r_gl, r_pscan], writes=[r_sf])
                            P.act(lambda e, sf=sf, sn_=sn_: e.copy(out=sn_[:], in_=sf[:]), reads=[r_sf], writes=[r_sn])
                            scur[hh] = 1 - scur[hh]
                        P.pe(lambda e, qkT=qkT, vn=vn: e.matmul(pR[:, 384:512], lhsT=qkT[:], rhs=vn[:], start=False, stop=True),
                             reads=[r_qkT, r_vn], writes=[r_pR])
                        junk, r_junk = Bt["junk"]
                        og_, r_og = Bt["og"]
                        P.act(lambda e, junk=junk: e.activation(out=junk[:], in_=pR[:, 384:512], func=AF.Square,
                                                                accum_out=st[:, 0:1]), reads=[r_pR], writes=[r_junk, r_st])
                        P.act(lambda e: e.activation(out=st[:, 1:2], in_=st[:, 0:1], func=AF.Ln, scale=1.0 / 128,
                                                     bias=cst[:, 5:6]), reads=[r_st, r_cst], writes=[r_st])
                        P.act(lambda e: e.activation(out=st[:, 2:3], in_=st[:, 1:2], func=AF.Exp, scale=-0.5),
                              reads=[r_st], writes=[r_st])
                        P.dve(lambda e, og_=og_, tt=tt, hh=hh: e.scalar_tensor_tensor(
                            out=og_[:], in0=pR[:, 384:512], scalar=st[:, 2:3], in1=zw[:, tt, hh * 128:(hh + 1) * 128],
                            op0=ALU.mult, op1=ALU.mult), reads=[r_pR, r_st, r_zw], writes=[r_og])
                        P.pe(lambda e, og_=og_: e.transpose(out=pt[:, 384:512], in_=og_[:], identity=identb[:]),
                             reads=[r_og, r_identb], writes=[r_pt])
                        P.act(lambda e, ob=ob, hh=hh, c0=c0: e.copy(out=ob[:, hh, c0:c0 + 128], in_=pt[:, 384:512]),
                              reads=[r_pt], writes=[r_ob])
                P.dma("pool", ogT_v[:, :, tok0:tok0 + 512], ob[:], [r_ob], [r_ogT], r_ob, is_output=True)
        P.finalize(stack)
    return nc


def phase_gdn(hnT_full, w_in, conv_w, a_log, dt_bias, out_norm_w, nblk_dbg=None):
    nc = get_prog("gdn", lambda: build_gdn(nblk_dbg))
    in_maps = []
    for c in range(NCORES):
        sel = np.concatenate([head_cols(c, 0), head_cols(c, 2048), head_cols(c, 4096)])
        wqkv = w_in[:, sel]
        gcols = np.array([8192 + 2 * c, 8192 + 2 * c + 1, 8208 + 2 * c, 8208 + 2 * c + 1])
        wzg = np.concatenate([w_in[:, head_cols(c, 6144)], w_in[:, gcols]], axis=1)
        cw = conv_w[:, sel].reshape(4, 6, 128).transpose(2, 1, 0).reshape(128, 24)
        gp = np.concatenate([a_log[2 * c:2 * c + 2], dt_bias[2 * c:2 * c + 2]]).reshape(1, 4)
        onw = np.concatenate([out_norm_w, out_norm_w]).reshape(1, 256)
        in_maps.append({"hnT": hnT_full, "wqkv": np.ascontiguousarray(wqkv),
                        "wzg": np.ascontiguousarray(wzg), "cw": np.ascontiguousarray(cw),
                        "gp": np.ascontiguousarray(gp.astype(np.float32)),
                        "onw": np.ascontiguousarray(onw.astype(np.float32))})
    res = run_spmd(nc, in_maps)
    return np.concatenate([r_["ogT"] for r_ in res], axis=0)


def kernel(x, norm_w, a_w_in, a_conv_w, a_a_log, a_dt_bias, a_out_norm_w, a_w_out,
           b_w_in, b_q_norm_w, b_k_norm_w, b_rel_bias, b_w_out):
    x = np.asarray(x, dtype=np.float32)
    x2d = x.reshape(NTOK, D)
    f = lambda a: np.asarray(a, dtype=np.float32)
    hnT = phase0(x2d, f(norm_w)[0])
    ogA = phase_gdn(np.ascontiguousarray(hnT), f(a_w_in)[0], f(a_conv_w)[0], f(a_a_log)[0],
                    f(a_dt_bias)[0], f(a_out_norm_w)[0])
    h1, hn1T = phase_outproj(ogA, f(a_w_out)[0], x2d, f(norm_w)[1])
    ogB = phase_attn(np.ascontiguousarray(hn1T), f(b_w_in)[0], f(b_q_norm_w)[0], f(b_k_norm_w)[0],
                     f(b_rel_bias)[0])
    out = phase_outproj(ogB, f(b_w_out)[0], h1)
    return out.reshape(B, T, D).astype(np.float32)
```
